# Optimizing a Trainium2 kernel written in Bass

```python
import math
import jax, jax.numpy as jnp
from jax import lax
import numpy as np

D_MODEL = 1024
BATCH = 4
SEQ = 8192
DEPTH = 4

PLE_DIM = 256
N_BRANCH = 4
BRANCH_W = D_MODEL // N_BRANCH
N_IN_BLOCKS = 10
CONV_A_WIDTH = 3
ATT_HEADS = 4
ATT_HEAD_DIM = BRANCH_W // ATT_HEADS
DSW_GROUPS = ((128, 1), (512, 4), (2048, 16))
SGU_CHUNK = 128
SGU_GROUPS = 4
CONF_KERNEL = 31
FFN_HIDDEN = -(-8 * D_MODEL // (3 * 256)) * 256
EPS = 1e-6

kernel_name = 'hybrid_parallel_gated_mixers'


def rmsnorm(x, g):
    xf = x.astype(jnp.float32)
    y = xf * lax.rsqrt(jnp.mean(xf * xf, axis=-1, keepdims=True) + EPS)
    return (y * g.astype(jnp.float32)).astype(x.dtype)


def layernorm(x, g, b):
    xf = x.astype(jnp.float32)
    mu = jnp.mean(xf, axis=-1, keepdims=True)
    var = jnp.mean(jnp.square(xf - mu), axis=-1, keepdims=True)
    y = (xf - mu) * lax.rsqrt(var + EPS)
    return (y * g.astype(jnp.float32) + b.astype(jnp.float32)).astype(x.dtype)


def causal_dwconv(x, w):
    k = w.shape[0]
    return lax.conv_general_dilated(
        x, w[:, None, :].astype(x.dtype), window_strides=(1,), padding=[(k - 1, 0)],
        dimension_numbers=('NWC', 'WIO', 'NWC'), feature_group_count=x.shape[-1])


def dilated_window_group(q, k, v, window, dilation):
    bsz, nh, s, dh = q.shape
    blk = window // dilation
    span = blk * dilation
    sp = -(-s // span) * span
    nb = sp // span

    def to_blocks(t):
        t = jnp.pad(t, ((0, 0), (0, 0), (0, sp - s), (0, 0)))
        t = t.reshape(bsz, nh, sp // dilation, dilation, dh).transpose(0, 1, 3, 2, 4)
        return t.reshape(bsz, nh, dilation, nb, blk, dh)

    def with_prev(t):
        prev = jnp.pad(t, ((0, 0), (0, 0), (0, 0), (1, 0), (0, 0), (0, 0)))[:, :, :, :-1]
        return jnp.concatenate([prev, t], axis=-2)

    qb = to_blocks(q)
    kc = with_prev(to_blocks(k))
    vc = with_prev(to_blocks(v))
    scores = jnp.einsum('bhrnqc,bhrnkc->bhrnqk', qb, kc).astype(jnp.float32) * (dh ** -0.5)
    qi = jnp.arange(blk)[:, None]
    ki = jnp.arange(2 * blk)[None, :]
    dist = qi + blk - ki
    band = (dist >= 0) & (dist <= blk)
    not_before_start = (jnp.arange(nb) > 0)[:, None, None] | (ki >= blk)[None]
    mask = band[None] & not_before_start
    scores = jnp.where(mask, scores, -jnp.inf)
    m = jnp.max(scores, axis=-1, keepdims=True)
    e = jnp.exp(scores - m)
    l = jnp.sum(e, axis=-1, keepdims=True)
    o = jnp.einsum('bhrnqk,bhrnkc->bhrnqc', e, vc.astype(jnp.float32)) / l
    lse = m + jnp.log(l)

    def from_blocks(t):
        c = t.shape[-1]
        t = t.reshape(bsz, nh, dilation, sp // dilation, c).transpose(0, 1, 3, 2, 4)
        return t.reshape(bsz, nh, sp, c)[:, :, :s]

    return from_blocks(o), from_blocks(lse)[..., 0]


def dilated_attention(q, k, v):
    bsz, s, _ = q.shape
    heads = lambda t: t.reshape(bsz, s, ATT_HEADS, ATT_HEAD_DIM).transpose(0, 2, 1, 3)
    qh, kh, vh = heads(q), heads(k), heads(v)
    outs, lses = [], []
    for window, dilation in DSW_GROUPS:
        o_g, lse_g = dilated_window_group(qh, kh, vh, window, dilation)
        outs.append(o_g)
        lses.append(lse_g)
    wts = jax.nn.softmax(jnp.stack(lses), axis=0)
    o = sum(wts[g][..., None] * outs[g] for g in range(len(DSW_GROUPS)))
    return o.transpose(0, 2, 1, 3).reshape(bsz, s, BRANCH_W).astype(q.dtype)


def spatial_gating(u, v, ln_g, ln_b, w_s, b_s):
    bsz, s, c = v.shape
    v = layernorm(v, ln_g, ln_b)
    vb = v.reshape(bsz, s // SGU_CHUNK, SGU_CHUNK, SGU_GROUPS, c // SGU_GROUPS)
    causal = jnp.tril(jnp.ones((SGU_CHUNK, SGU_CHUNK), dtype=bool))
    w = jnp.where(causal[None], w_s, jnp.zeros_like(w_s))
    mixed = jnp.einsum('gts,bnsgc->bntgc', w, vb) + b_s.T[None, None, :, :, None]
    return u * mixed.reshape(bsz, s, c)


def conformer_conv(val, gate, dw, ln_g, ln_b):
    y = val * jax.nn.sigmoid(gate)
    y = causal_dwconv(y, dw)
    y = layernorm(y, ln_g, ln_b)
    return jax.nn.silu(y)


def setup_inputs(seed: int = 0) -> dict:
    key = jax.random.key(seed)
    ks = jax.random.split(key, 22)
    nrm = lambda k, shape: jax.random.normal(k, shape, jnp.float32)
    res = (2.0 * DEPTH) ** -0.5
    bw = BRANCH_W
    return {
        'x': nrm(ks[0], (BATCH, SEQ, D_MODEL)),
        'p': nrm(ks[1], (DEPTH, BATCH, SEQ, PLE_DIM)),
        'g_mix': 1.0 + 0.02 * nrm(ks[2], (DEPTH, D_MODEL)),
        'w_in': nrm(ks[3], (DEPTH, D_MODEL, N_IN_BLOCKS * bw)) * D_MODEL ** -0.5,
        'conv_a': nrm(ks[4], (DEPTH, CONV_A_WIDTH, bw)) * CONV_A_WIDTH ** -0.5,
        'sgu_ln_g': 1.0 + 0.02 * nrm(ks[5], (DEPTH, bw)),
        'sgu_ln_b': 0.02 * nrm(ks[6], (DEPTH, bw)),
        'sgu_w': nrm(ks[7], (DEPTH, SGU_GROUPS, SGU_CHUNK, SGU_CHUNK)) * SGU_CHUNK ** -0.5,
        'sgu_b': 1.0 + 0.02 * nrm(ks[8], (DEPTH, SGU_GROUPS, SGU_CHUNK)),
        'conf_dw': nrm(ks[9], (DEPTH, CONF_KERNEL, bw)) * CONF_KERNEL ** -0.5,
        'conf_ln_g': 1.0 + 0.02 * nrm(ks[10], (DEPTH, bw)),
        'conf_ln_b': 0.02 * nrm(ks[11], (DEPTH, bw)),
        'w_branch': nrm(ks[12], (DEPTH, N_BRANCH, bw, D_MODEL)) * bw ** -0.5,
        'w_merge_gate': nrm(ks[13], (DEPTH, N_BRANCH, D_MODEL, D_MODEL)) * D_MODEL ** -0.5,
        'w_out': nrm(ks[14], (DEPTH, D_MODEL, D_MODEL)) * D_MODEL ** -0.5 * res,
        'g_ffn': 1.0 + 0.02 * nrm(ks[15], (DEPTH, D_MODEL)),
        'w_ffn_in': nrm(ks[16], (DEPTH, D_MODEL, 2 * FFN_HIDDEN)) * D_MODEL ** -0.5,
        'w_ffn_out': nrm(ks[17], (DEPTH, FFN_HIDDEN, D_MODEL)) * FFN_HIDDEN ** -0.5 * res,
        'g_ple': 1.0 + 0.02 * nrm(ks[18], (DEPTH, D_MODEL)),
        'w_ple_gate': nrm(ks[19], (DEPTH, D_MODEL, D_MODEL)) * D_MODEL ** -0.5,
        'w_ple_proj': nrm(ks[20], (DEPTH, PLE_DIM, D_MODEL)) * PLE_DIM ** -0.5,
        'g_final': 1.0 + 0.02 * nrm(ks[21], (D_MODEL,)),
    }


def reference(x, p, g_mix, w_in, conv_a, sgu_ln_g, sgu_ln_b, sgu_w, sgu_b, conf_dw,
              conf_ln_g, conf_ln_b, w_branch, w_merge_gate, w_out, g_ffn, w_ffn_in,
              w_ffn_out, g_ple, w_ple_gate, w_ple_proj, g_final):
    for i in range(DEPTH):
        h = rmsnorm(x, g_mix[i])
        proj = h @ w_in[i]
        (a_b, a_c, a_x, q, k, v, s_u, s_v, c_val, c_gate) = jnp.split(proj, N_IN_BLOCKS, axis=-1)
        y_a = a_b * causal_dwconv(a_c * a_x, conv_a[i])
        y_b = dilated_attention(q, k, v)
        y_c = spatial_gating(s_u, s_v, sgu_ln_g[i], sgu_ln_b[i], sgu_w[i], sgu_b[i])
        y_d = conformer_conv(c_val, c_gate, conf_dw[i], conf_ln_g[i], conf_ln_b[i])
        branches = (y_a, y_b, y_c, y_d)
        merged = sum(jax.nn.sigmoid(h @ w_merge_gate[i, br]) * (branches[br] @ w_branch[i, br])
                     for br in range(N_BRANCH))
        x = x + merged @ w_out[i]
        h2 = rmsnorm(x, g_ffn[i])
        f_gate, f_up = jnp.split(h2 @ w_ffn_in[i], 2, axis=-1)
        x = x + (jax.nn.silu(f_gate) * f_up) @ w_ffn_out[i]
        h3 = rmsnorm(x, g_ple[i])
        x = x + jax.nn.sigmoid(h3 @ w_ple_gate[i]) * (p[i].astype(x.dtype) @ w_ple_proj[i])
    return rmsnorm(x, g_final)
```

```python
from contextlib import ExitStack
import numpy as np
import concourse.bass as bass
import concourse.mybir as mybir
from concourse.bass_utils import run_bass_kernel_spmd

F32 = mybir.dt.float32
BF16 = mybir.dt.bfloat16
ALU = mybir.AluOpType
AF = mybir.ActivationFunctionType

ENGS = ("sync", "scalar", "vector", "gpsimd", "tensor")
EPOCH = 30000

D = 1024
BW = 256
DEPTH = 4
BATCH = 4
SEQ = 8192
PLE = 256
FH = 2816
NJ = FH // 128
TC = 4096
U = 2048
NU = TC // U
NT = U // 128
NTT = U // 512
EPS = 1e-6
VW = 4 * 65


class Res:
    __slots__ = ("name", "lw", "rd", "dsem", "dcount", "persist", "dq")

    def __init__(self, name, persist=False):
        self.name = name
        self.lw = None
        self.rd = {}
        self.dsem = None
        self.dcount = 0
        self.persist = persist
        self.dq = None


class Prog:
    def __init__(self, nc, stack):
        self.nc = nc
        self.stack = stack
        self.ops = {e: [] for e in ENGS}
        self.ecount = {e: 0 for e in ENGS}
        self.esem = {e: None for e in ENGS}
        self.seen = {e: {} for e in ENGS}
        self.nsem = 0
        self.ninst = 0
        self.dslots = []
        self.old_epochs = []
        self.max_ops = None
        self.log = []
        self.free_dsems = {e: [] for e in ENGS}

    def _newsem(self, name):
        self.nsem += 1
        return self.stack.enter_context(self.nc.semaphore(f"{name}_{self.nsem}"))

    def _deps(self, eng, reads, writes):
        need = {}
        for r in reads:
            if r.lw is not None:
                k, v = r.lw
                if need.get(k, 0) < v:
                    need[k] = v
        for w in writes:
            if w.lw is not None:
                k, v = w.lw
                if need.get(k, 0) < v:
                    need[k] = v
            for k, v in w.rd.items():
                if need.get(k, 0) < v:
                    need[k] = v
        waits = []
        seen = self.seen[eng]
        for k, v in need.items():
            if eng == "tensor" and k is self.esem["tensor"]:
                continue
            if seen.get(k, 0) >= v:
                continue
            seen[k] = v
            waits.append((k, v))
        return waits

    def _commit(self, ev, reads, writes):
        k, v = ev
        for r in reads:
            if r.rd.get(k, 0) < v:
                r.rd[k] = v
        for w in writes:
            w.lw = ev
            w.rd = {}

    def op(self, eng, fn, reads=(), writes=(), tag=""):
        if self.max_ops is not None and self.ninst >= self.max_ops:
            return
        self.log.append((self.ninst, eng, tag, [r.name for r in reads], [w.name for w in writes]))
        waits = self._deps(eng, reads, writes)
        if self.esem[eng] is None or self.ecount[eng] >= EPOCH:
            if self.esem[eng] is not None:
                self.old_epochs.append((self.esem[eng], self.ecount[eng]))
            self.esem[eng] = self._newsem(f"e_{eng}")
            self.ecount[eng] = 0
        self.ecount[eng] += 1
        self._commit((self.esem[eng], self.ecount[eng]), reads, writes)
        self.ops[eng].append((waits, fn, self.esem[eng], 1))
        self.ninst += 1

    def dma(self, eng, out, in_, slot, reads=(), writes=(), **kw):
        if self.max_ops is not None and self.ninst >= self.max_ops:
            return
        self.log.append((self.ninst, eng, "dma", [r.name for r in reads], [w.name for w in writes]))
        waits = self._deps(eng, reads, writes)
        if slot.dsem is None:
            if self.free_dsems[eng]:
                slot.dsem, slot.dcount = self.free_dsems[eng].pop()
            else:
                slot.dsem, slot.dcount = self._newsem("dma"), 0
            slot.dq = eng
            self.dslots.append(slot)
        assert slot.dq == eng, (slot.name, slot.dq, eng)
        slot.dcount += 16
        self._commit((slot.dsem, slot.dcount), reads, writes)
        self.ops[eng].append((waits, lambda e: e.dma_start(out=out, in_=in_, **kw), slot.dsem, 16))
        self.ninst += 1

    def wait_all(self, eng, resources):
        waits = self._deps(eng, (), resources)
        if waits:
            self.ops[eng].append((waits, None, None, 0))

    def full_barrier(self):
        evs = [(self.esem[e], self.ecount[e]) for e in ENGS if self.esem[e] is not None]
        evs += [(s.dsem, s.dcount) for s in self.dslots]
        evs += self.old_epochs
        self.old_epochs = []
        for e in ENGS:
            seen = self.seen[e]
            waits = []
            for k, v in evs:
                if seen.get(k, 0) >= v:
                    continue
                seen[k] = v
                waits.append((k, v))
            if waits:
                self.ops[e].append((waits, None, None, 0))
        keep = []
        for s in self.dslots:
            if s.persist:
                keep.append(s)
            else:
                self.free_dsems[s.dq].append((s.dsem, s.dcount))
                s.dsem = None
        self.dslots = keep

    def emit(self):
        ops = self.ops
        self.ops = {e: [] for e in ENGS}
        with self.nc.Block() as block:
            def run(name):
                def f(e):
                    for waits, fn, sem, inc in ops[name]:
                        for k, v in waits:
                            e.wait_ge(k, v)
                        if fn is not None:
                            fn(e).then_inc(sem, inc)
                return f
            block.sync(run("sync"))
            block.scalar(run("scalar"))
            block.vector(run("vector"))
            block.gpsimd(run("gpsimd"))
            block.tensor(run("tensor"))


class Ring:
    def __init__(self, mk, name, shape, dt, n):
        self.t = [mk(f"{name}{i}", shape, dt) for i in range(n)]
        self.r = [Res(f"{name}{i}") for i in range(n)]
        self.i = 0
        self.n = n

    def next(self):
        i = self.i % self.n
        self.i += 1
        return self.t[i], self.r[i]


class Ctx:
    pass


def _wsrc(ap2d):
    return ap2d.rearrange("(k p) n -> p k n", p=128)


class _Stop(Exception):
    pass


def build_program(n_layers, final_norm, dbg=False, max_phases=None, max_ops=None):
    nc = bass.Bass("TRN2", target_bir_lowering=False)
    dt_in = lambda name, shape: nc.dram_tensor(name, shape, F32, kind="ExternalInput").ap()
    C = Ctx()
    C.nc = nc
    x_in = dt_in("x", [TC, D])
    xh_in = dt_in("xh", [2 * U, D])
    flag_in = dt_in("flag", [128, 1])
    ident_in = dt_in("ident", [128, 128])
    matt_in = dt_in("matt", [128, 512])
    msgu_in = dt_in("msgu", [128, 128])
    gfin_in = dt_in("g_final", [1, D])
    W = []
    for l in range(n_layers):
        w = {}
        w["p"] = dt_in(f"p{l}", [TC, PLE])
        w["ph"] = dt_in(f"ph{l}", [2 * U, PLE])
        w["w_in"] = dt_in(f"w_in{l}", [D, 10 * BW])
        w["conv_aT"] = dt_in(f"conv_aT{l}", [BW, 3])
        w["sgu_ln_g"] = dt_in(f"sgu_ln_g{l}", [1, BW])
        w["sgu_ln_b"] = dt_in(f"sgu_ln_b{l}", [1, BW])
        w["sgu_wT"] = dt_in(f"sgu_wT{l}", [4, 128, 128])
        w["sgu_b"] = dt_in(f"sgu_b{l}", [1, 512])
        w["conf_dwT"] = dt_in(f"conf_dwT{l}", [BW, 31])
        w["conf_ln_g"] = dt_in(f"conf_ln_g{l}", [128, 2])
        w["conf_ln_b"] = dt_in(f"conf_ln_b{l}", [128, 2])
        w["w_branch"] = dt_in(f"w_branch{l}", [4, BW, D])
        w["w_gate"] = dt_in(f"w_gate{l}", [4, D, D])
        w["w_out"] = dt_in(f"w_out{l}", [D, D])
        w["g_mix"] = dt_in(f"g_mix{l}", [1, D])
        w["g_ffn"] = dt_in(f"g_ffn{l}", [1, D])
        w["g_ple"] = dt_in(f"g_ple{l}", [1, D])
        w["w_ffn_in"] = dt_in(f"w_ffn_in{l}", [D, 2 * FH])
        w["w_ffn_out"] = dt_in(f"w_ffn_out{l}", [FH, D])
        w["w_ple_gate"] = dt_in(f"w_ple_gate{l}", [D, D])
        w["w_ple_proj"] = dt_in(f"w_ple_proj{l}", [PLE, D])
        W.append(w)
    y_out = nc.dram_tensor("y", [TC, D], F32, kind="ExternalOutput").ap()
    xhs = nc.dram_tensor("xhs", [2 * U, D], F32, kind="ExternalOutput").ap()
    vs = nc.dram_tensor("vs", [5 * U, VW], BF16, kind="Internal").ap()
    NSLOT = 4
    r_in = Res("inputs", True)
    r_X = [[Res(f"X{sl}_{t}", True) for t in range(NT)] for sl in range(NSLOT)]
    r_vs = Res("vs", True)

    def xrows(sl):
        if sl < 2:
            return xhs[sl * U:(sl + 1) * U, :], r_X[sl]
        return y_out[(sl - 2) * U:(sl - 1) * U, :], r_X[sl]

    def xin_rows(sl):
        if sl < 2:
            return xh_in[sl * U:(sl + 1) * U, :], [r_in] * NT
        return x_in[(sl - 2) * U:(sl - 1) * U, :], [r_in] * NT
    C.dbg = {}
    if dbg:
        for name, shape in (("d_hT", [128, 8 * 512]), ("d_ya", [128, 2 * 512]), ("d_yb", [64, 4 * 512]),
                            ("d_yc", [128, 2 * 512]), ("d_yd", [128, 2 * 512]), ("d_m", [128, 8 * 512])):
            C.dbg[name] = nc.dram_tensor(name, shape, F32, kind="ExternalOutput").ap()
    r_dbg = Res("dbg", True)

    with ExitStack() as st:
        P = Prog(nc, st)
        P.max_ops = max_ops
        C.P = P

        uid = [0]

        def mk_sb(stack):
            def f(name, shape, dt):
                uid[0] += 1
                return stack.enter_context(nc.sbuf_tensor(f"{name}_{uid[0]}", shape, dt))
            return f

        def mk_ps(stack):
            def f(name, shape, dt):
                uid[0] += 1
                return stack.enter_context(nc.psum_tensor(f"{name}_{uid[0]}", shape, dt))
            return f

        sb = mk_sb(st)
        identf = sb("identf", [128, 128], F32)
        identb = sb("identb", [128, 128], BF16); r_identb = Res("identb", True)
        matt = sb("matt", [128, 4, 128], BF16); r_matt = Res("matt", True)
        msgu = sb("msgu", [128, 128], F32); r_msgu = Res("msgu", True)
        onesm = sb("onesm", [128, 128], F32); r_onesm = Res("onesm", True)
        sel = sb("sel", [128, 64], F32); r_sel = Res("sel", True)
        flagt = sb("flagt", [128, 1], F32); r_flag = Res("flag", True)
        epst = sb("epst", [128, 1], F32); r_epst = Res("epst", True)
        hT = sb("hT", [128, 8, U], BF16); r_hT = Res("hT", True)
        Kb = [sb(f"Kb{i}", [128, 2, U], BF16) for i in range(2)]
        r_Kb = [Res(f"Kb{i}", True) for i in range(2)]
        za_tail = sb("za_tail", [128, 2, 2], F32); r_zat = Res("za_tail", True)
        zd_tail = sb("zd_tail", [128, 2, 30], F32); r_zdt = Res("zd_tail", True)
        gmix = sb("gmix", [128, D], F32); r_gmix = Res("gmix", True)
        gffn = sb("gffn", [128, D], F32); r_gffn = Res("gffn", True)
        gple = sb("gple", [128, D], F32); r_gple = Res("gple", True)
        caT = sb("caT", [128, 2, 3], F32); r_caT = Res("caT", True)
        dwT = sb("dwT", [128, 2, 31], F32); r_dwT = Res("dwT", True)
        clng = sb("clng", [128, 2], F32); r_clng = Res("clng", True)
        clnb = sb("clnb", [128, 2], F32); r_clnb = Res("clnb", True)
        slng = sb("slng", [128, BW], F32); r_slng = Res("slng", True)
        slnb = sb("slnb", [128, BW], F32); r_slnb = Res("slnb", True)
        sgub = sb("sgub", [128, 4, 128], F32); r_sgub = Res("sgub", True)
        wsTf = sb("wsTf", [128, 4, 128], F32); r_wsTf = Res("wsTf", True)
        wsT = sb("wsT", [128, 4, 128], BF16); r_wsT = Res("wsT", True)

        def bcast(ap_row, n):
            return bass.AP(ap_row.tensor, ap_row.offset, [[0, 128], [1, n]])

        r_tmp = Res("identf", True)
        P.dma("sync", identf[:], ident_in, r_tmp, writes=[r_tmp])
        P.op("vector", lambda e: e.tensor_copy(out=identb[:], in_=identf[:]), reads=[r_tmp], writes=[r_identb])
        P.dma("gpsimd", matt[:].rearrange("p a b -> p (a b)"), matt_in, r_matt, writes=[r_matt])
        P.dma("sync", msgu[:], msgu_in, r_msgu, writes=[r_msgu])
        P.dma("sync", flagt[:], flag_in, r_flag, writes=[r_flag])
        P.op("vector", lambda e: e.memset(onesm[:], 1.0 / 256.0), writes=[r_onesm])
        P.op("vector", lambda e: e.memset(epst[:], 1e-30), writes=[r_epst])
        P.op("vector", lambda e: e.memset(sel[:], 0.0), writes=[r_sel])
        P.op("vector", lambda e: e.memset(sel[64:65, :], 1.0), writes=[r_sel])

        def norm_A(N, xt, r_x):
            sq, r_sq = N.sq.next()
            ss, r_ss = N.ss.next()
            P.op("scalar", lambda e: e.activation(out=sq[:], in_=xt, func=AF.Square, scale=1.0 / 32.0, accum_out=ss[:]),
                 reads=[r_x], writes=[r_sq, r_ss])
            P.op("vector", lambda e: e.tensor_scalar_add(out=ss[:], in0=ss[:], scalar1=EPS), reads=[r_ss], writes=[r_ss])
            P.op("vector", lambda e: e.reciprocal(out=ss[:], in_=ss[:]), reads=[r_ss], writes=[r_ss])
            P.op("scalar", lambda e: e.activation(out=ss[:], in_=ss[:], func=AF.Sqrt), reads=[r_ss], writes=[r_ss])
            return ss, r_ss

        def norm_B(N, xt, r_x, ss, r_ss, gt, r_g):
            hb, r_hb = N.hb.next()
            P.op("vector", lambda e: e.scalar_tensor_tensor(out=hb[:], in0=xt, scalar=ss[:, 0:1], in1=gt[:],
                                                            op0=ALU.mult, op1=ALU.mult),
                 reads=[r_x, r_ss, r_g], writes=[r_hb])
            pt, r_pt = N.pt.next()
            for k in range(8):
                P.op("tensor", lambda e, k=k: e.transpose(out=pt[:, k, :], in_=hb[:, k * 128:(k + 1) * 128], identity=identb[:]),
                     reads=[r_hb, r_identb], writes=[r_pt])
            return pt, r_pt

        def norm_C(pt, r_pt, col):
            P.op("scalar", lambda e: e.copy(out=hT[:, :, col:col + 128], in_=pt[:]), reads=[r_pt], writes=[r_hT])

        def skew(n, stages, after_step=None):
            st_ = {}
            for step in range(n + len(stages) - 1):
                for k, fn in enumerate(stages):
                    t = step - k
                    if 0 <= t < n:
                        st_[(k, t)] = fn(t, st_.get((k - 1, t)))
                if after_step is not None:
                    after_step()

        class NormBufs:
            def __init__(self, sbf, psf):
                self.sq = Ring(sbf, "n_sq", [128, D], BF16, 2)
                self.ss = Ring(sbf, "n_ss", [128, 1], F32, 4)
                self.hb = Ring(sbf, "n_hb", [128, D], BF16, 3)
                self.pt = Ring(psf, "n_pt", [128, 8, 128], BF16, 3)

        nph = [0]
        stopped = [False]

        def end_phase():
            P.full_barrier()
            P.emit()
            nph[0] += 1
            if max_phases is not None and nph[0] >= max_phases:
                stopped[0] = True

        def phase_norm(src, rs_src, ntiles, gt, r_g, use_flag=False):
            if stopped[0]:
                return
            with ExitStack() as ph:
                sbf, psf = mk_sb(ph), mk_ps(ph)
                N = NormBufs(sbf, psf)
                xr = Ring(sbf, "n_x", [128, D], F32, 4)

                def sA(t, _):
                    xt, r_x = xr.next()
                    P.dma("sync", xt[:], src[t * 128:(t + 1) * 128, :], r_x, reads=[rs_src[t]], writes=[r_x])
                    if use_flag:
                        P.op("vector", lambda e: e.tensor_scalar(out=xt[:], in0=xt[:], scalar1=flagt[:, 0:1], scalar2=None, op0=ALU.mult),
                             reads=[r_x, r_flag], writes=[r_x])
                    ss, r_ss = norm_A(N, xt[:], r_x)
                    return (xt, r_x, ss, r_ss)

                def sB(t, prev):
                    xt, r_x, ss, r_ss = prev
                    return norm_B(N, xt[:], r_x, ss, r_ss, gt, r_g)

                def sC(t, prev):
                    norm_C(prev[0], prev[1], t * 128)

                skew(ntiles, [sA, sB, sC])
                end_phase()

        def load_layer_small(w):
            P.dma("sync", gmix[:], bcast(w["g_mix"], D), r_gmix, writes=[r_gmix])
            P.dma("sync", gffn[:], bcast(w["g_ffn"], D), r_gffn, writes=[r_gffn])
            P.dma("sync", gple[:], bcast(w["g_ple"], D), r_gple, writes=[r_gple])
            P.dma("sync", caT[:], w["conv_aT"].rearrange("(c p) k -> p c k", p=128), r_caT, writes=[r_caT])
            P.dma("sync", dwT[:], w["conf_dwT"].rearrange("(c p) k -> p c k", p=128), r_dwT, writes=[r_dwT])
            P.dma("sync", clng[:], w["conf_ln_g"], r_clng, writes=[r_clng])
            P.dma("sync", clnb[:], w["conf_ln_b"], r_clnb, writes=[r_clnb])
            P.dma("sync", slng[:], bcast(w["sgu_ln_g"], BW), r_slng, writes=[r_slng])
            P.dma("sync", slnb[:], bcast(w["sgu_ln_b"], BW), r_slnb, writes=[r_slnb])
            P.dma("sync", sgub[:].rearrange("p a b -> p (a b)"), bcast(w["sgu_b"], 512), r_sgub, writes=[r_sgub])
            P.dma("sync", wsTf[:], w["sgu_wT"].rearrange("g s t -> s g t"), r_wsTf, writes=[r_wsTf])
            for g in range(4):
                P.op("vector", lambda e, g=g: e.tensor_tensor(out=wsT[:, g, :], in0=wsTf[:, g, :], in1=msgu[:], op=ALU.mult),
                     reads=[r_wsTf, r_msgu], writes=[r_wsT])

        def load_wblk(M, w, b):
            wt, r_w = M.wblk.next()
            P.dma("gpsimd", wt[:], _wsrc(w["w_in"][:, b * BW:(b + 1) * BW]), r_w, writes=[r_w])
            return wt, r_w

        def proj_fm(M, wt, r_w, c, col0, ncols):
            ps, r_ps = M.pp.next()
            for k in range(8):
                P.op("tensor", lambda e, k=k: e.matmul(out=ps[:, 0:ncols], lhsT=wt[:, k, c * 128:(c + 1) * 128],
                                                      rhs=hT[:, k, col0:col0 + ncols], start=(k == 0), stop=(k == 7)),
                     reads=[r_w, r_hT], writes=[r_ps])
            return ps, r_ps

        def proj_tm(M, wt, r_w, t):
            ps, r_ps = M.pp.next()
            for k in range(8):
                P.op("tensor", lambda e, k=k: e.matmul(out=ps[:, 0:BW], lhsT=hT[:, k, t * 128:(t + 1) * 128],
                                                      rhs=wt[:, k, :], start=(k == 0), stop=(k == 7)),
                     reads=[r_w, r_hT], writes=[r_ps])
            return ps, r_ps

        def init_ones(vring, use_flag):
            for i in range(vring.n):
                if use_flag:
                    for h in range(4):
                        P.op("vector", lambda e, i=i, h=h: e.tensor_copy(out=vring.t[i][:, h, 64:65], in_=flagt[:, 0:1]),
                             reads=[r_flag], writes=[vring.r[i]])
                else:
                    P.op("vector", lambda e, i=i: e.memset(vring.t[i][:, :, 64:65], 1.0), writes=[vring.r[i]])

        def kv_proj(M, w, Kdst, r_K, vrow0, vring, do_q=None):
            wk, r_wk = load_wblk(M, w, 4)
            wv, r_wv = load_wblk(M, w, 5)
            if do_q is not None:
                wq, r_wq = load_wblk(M, w, 3)
            for c in range(2):
                for tt in range(NTT):
                    ps, r_ps = proj_fm(M, wk, r_wk, c, tt * 512, 512)
                    P.op("scalar", lambda e, ps=ps, c=c, tt=tt: e.copy(out=Kdst[:, c, tt * 512:(tt + 1) * 512], in_=ps[:]),
                         reads=[r_ps], writes=[r_K])
            for t in range(NT):
                ps, r_ps = proj_tm(M, wv, r_wv, t)
                vt, r_vt = vring.next()
                P.op("vector", lambda e, ps=ps, vt=vt: e.tensor_copy(out=vt[:, :, 0:64], in_=ps[:, 0:BW].rearrange("p (h d) -> p h d", d=64)),
                     reads=[r_ps], writes=[r_vt])
                P.dma("sync", vs[vrow0 + t * 128: vrow0 + (t + 1) * 128, :], vt[:].rearrange("p h d -> p (h d)"), r_vt,
                      reads=[r_vt], writes=[r_vs])
            if do_q is not None:
                qT, r_q = do_q
                for c in range(2):
                    for tt in range(NTT):
                        ps, r_ps = proj_fm(M, wq, r_wq, c, tt * 512, 512)
                        P.op("scalar", lambda e, ps=ps, c=c, tt=tt: e.copy(out=qT[:, c, tt * 512:(tt + 1) * 512], in_=ps[:]),
                             reads=[r_ps], writes=[r_q])

        def mixer_a(M, w, c, tt, zbuf, r_z, wblks, ya=None, r_ya=None):
            (wb_, r_wb), (wc_, r_wc), (wx_, r_wx) = wblks
            pc, r_pc = proj_fm(M, wc_, r_wc, c, tt * 512, 512)
            px, r_px = proj_fm(M, wx_, r_wx, c, tt * 512, 512)
            xs, r_xs = M.tmpf.next()
            P.op("scalar", lambda e: e.copy(out=xs[:], in_=px[:]), reads=[r_px], writes=[r_xs])
            P.op("vector", lambda e: e.tensor_tensor(out=zbuf[:, 2:514], in0=pc[:], in1=xs[:], op=ALU.mult),
                 reads=[r_pc, r_xs], writes=[r_z])
            if ya is not None:
                pb_, r_pb = proj_fm(M, wb_, r_wb, c, tt * 512, 512)
                o, r_o = M.tmpf.next()
                P.op("vector", lambda e: e.tensor_scalar(out=o[:], in0=zbuf[:, 0:512], scalar1=caT[:, c, 0:1], scalar2=None, op0=ALU.mult),
                     reads=[r_z, r_caT], writes=[r_o])
                P.op("vector", lambda e: e.scalar_tensor_tensor(out=o[:], in0=zbuf[:, 1:513], scalar=caT[:, c, 1:2], in1=o[:],
                                                                op0=ALU.mult, op1=ALU.add), reads=[r_z, r_caT, r_o], writes=[r_o])
                P.op("vector", lambda e: e.scalar_tensor_tensor(out=o[:], in0=zbuf[:, 2:514], scalar=caT[:, c, 2:3], in1=o[:],
                                                                op0=ALU.mult, op1=ALU.add), reads=[r_z, r_caT, r_o], writes=[r_o])
                P.op("vector", lambda e: e.tensor_tensor(out=ya[:, c, tt * 512:(tt + 1) * 512], in0=o[:], in1=pb_[:], op=ALU.mult),
                     reads=[r_o, r_pb], writes=[r_ya])
            P.op("vector", lambda e: e.tensor_copy(out=zbuf[:, 0:2], in_=zbuf[:, 512:514]), reads=[r_z], writes=[r_z])

        def glu_d(M, c, tt, wblks, zd, r_zd):
            (wv_, r_wv), (wg_, r_wg) = wblks
            pv, r_pv = proj_fm(M, wv_, r_wv, c, tt * 512, 512)
            pg, r_pg = proj_fm(M, wg_, r_wg, c, tt * 512, 512)
            sg, r_sg = M.tmpf.next()
            P.op("scalar", lambda e: e.activation(out=sg[:], in_=pg[:], func=AF.Sigmoid), reads=[r_pg], writes=[r_sg])
            P.op("vector", lambda e: e.tensor_tensor(out=zd[:, 30 + tt * 512: 30 + (tt + 1) * 512], in0=pv[:], in1=sg[:], op=ALU.mult),
                 reads=[r_pv, r_sg], writes=[r_zd])

        def phase_halo(w, src, rs_src, use_flag, Kdst, r_K, vrow0):
            phase_norm(src, rs_src, NT, gmix, r_gmix, use_flag)
            if stopped[0]:
                return
            with ExitStack() as ph:
                sbf, psf = mk_sb(ph), mk_ps(ph)
                M = Ctx()
                M.wblk = Ring(sbf, "h_wblk", [128, 8, BW], BF16, 4)
                M.pp = Ring(psf, "h_pp", [128, 512], F32, 4)
                M.tmpf = Ring(sbf, "h_tmpf", [128, 512], F32, 3)
                vring = Ring(sbf, "h_vst", [128, 4, 65], BF16, 3)
                init_ones(vring, use_flag)
                kv_proj(M, w, Kdst, r_K, vrow0, vring)
                zb = sbf("h_zb", [128, 514], F32); r_zb = Res("h_zb")
                zd = sbf("h_zd", [128, 30 + U], F32); r_zd = Res("h_zd")
                wa = [load_wblk(M, w, 0), load_wblk(M, w, 1), load_wblk(M, w, 2)]
                for c in range(2):
                    P.op("vector", lambda e: e.memset(zb[:, 0:2], 0.0), writes=[r_zb])
                    mixer_a(M, w, c, NTT - 1, zb, r_zb, wa)
                    P.op("vector", lambda e, c=c: e.tensor_copy(out=za_tail[:, c, :], in_=zb[:, 0:2]), reads=[r_zb], writes=[r_zat])
                wd = [load_wblk(M, w, 8), load_wblk(M, w, 9)]
                for c in range(2):
                    glu_d(M, c, NTT - 1, wd, zd, r_zd)
                    P.op("vector", lambda e, c=c: e.tensor_copy(out=zd_tail[:, c, :], in_=zd[:, U:U + 30]), reads=[r_zd], writes=[r_zdt])
                end_phase()

        def phase_mixers(w, sl, Kcur, r_Kc, Kprev, r_Kp, Y):
            ya, r_ya, yb, r_yb, yc, r_yc, yd, r_yd = Y
            vrow_cur = U * (sl + 1)
            vrow_prev = U * sl
            if stopped[0]:
                return
            with ExitStack() as ph:
                sbf, psf = mk_sb(ph), mk_ps(ph)
                M = Ctx()
                M.wblk = Ring(sbf, "a_wblk", [128, 8, BW], BF16, 3)
                M.pp = Ring(psf, "a_pp", [128, 512], F32, 2)
                vring = Ring(sbf, "a_vst", [128, 4, 65], BF16, 3)
                init_ones(vring, sl < 2)
                qT = sbf("a_qT", [128, 2, U], BF16); r_q = Res("a_qT")
                kv_proj(M, w, Kcur, r_Kc, vrow_cur, vring, do_q=(qT, r_q))
                acc = sbf("a_acc", [65, 4, U], F32); r_acc = Res("a_acc")
                vtr = Ring(sbf, "a_vt", [128, 4, 65], BF16, 6)
                Sa = Ring(psf, "a_Sa", [128, 4, 128], F32, 2)
                Sb = Ring(psf, "a_Sb", [128, 4, 128], F32, 2)
                Er = Ring(sbf, "a_E", [128, 4, 128], BF16, 6)
                Or = Ring(psf, "a_O", [65, 4, 128], F32, 2)

                def load_v(base, d, r):
                    vt, r_vt = vtr.next()
                    src = vs[base: base + 128 * d, :].rearrange("(j d) c -> d j c", d=d)[r]
                    P.dma("sync", vt[:].rearrange("p h d -> p (h d)"), src, r_vt, reads=[r_vs], writes=[r_vt])
                    return vt, r_vt

                def cols(tile3, pb, c, n, span, d, r):
                    v = tile3[pb:pb + 64, c, n * span:(n + 1) * span]
                    if d == 1:
                        return v
                    return v.rearrange("p (j d) -> p j d", d=d)[:, :, r]

                blocks = []
                for gi, d in enumerate((1, 4, 16)):
                    span = 128 * d
                    nb = U // span
                    for r in range(d):
                        for n in range(nb):
                            blocks.append((gi, d, span, nb, r, n))
                vstate = {}

                def aA(b, _):
                    gi, d, span, nb, r, n = blocks[b]
                    if n == 0:
                        vt_p, r_vp = load_v(vrow_prev + (nb - 1) * span, d, r)
                    else:
                        vt_p, r_vp = vstate["last"]
                    vt_c, r_vc = load_v(vrow_cur + n * span, d, r)
                    vstate["last"] = (vt_c, r_vc)
                    Sab = (Sa.next(), Sb.next())
                    for c in range(2):
                        for hh in range(2):
                            pb = 64 * hh
                            S, r_S = Sab[hh]
                            qv = cols(qT, pb, c, n, span, d, r)
                            if n == 0:
                                kpv = cols(Kprev, pb, c, nb - 1, span, d, r); r_kpv = r_Kp
                            else:
                                kpv = cols(Kcur, pb, c, n - 1, span, d, r); r_kpv = r_Kc
                            kcv = cols(Kcur, pb, c, n, span, d, r)
                            P.op("tensor", lambda e, S=S, c=c, kpv=kpv, qv=qv: e.matmul(out=S[:, 2 * c, :], lhsT=kpv, rhs=qv, start=True, stop=True),
                                 reads=[r_kpv, r_q], writes=[r_S])
                            P.op("tensor", lambda e, S=S, c=c, kcv=kcv, qv=qv: e.matmul(out=S[:, 2 * c + 1, :], lhsT=kcv, rhs=qv, start=True, stop=True),
                                 reads=[r_Kc, r_q], writes=[r_S])
                    Es = []
                    for hh in range(2):
                        S, r_S = Sab[hh]
                        E, r_E = Er.next()
                        P.op("scalar", lambda e, S=S, E=E: e.activation(out=E[:], in_=S[:], func=AF.Exp, scale=0.125),
                             reads=[r_S], writes=[r_E])
                        P.op("vector", lambda e, E=E: e.tensor_tensor(out=E[:], in0=E[:], in1=matt[:], op=ALU.mult),
                             reads=[r_E, r_matt], writes=[r_E])
                        Es.append((E, r_E))
                    return Es, vt_p, r_vp, vt_c, r_vc

                def aB(b, prev):
                    gi, d, span, nb, r, n = blocks[b]
                    Es, vt_p, r_vp, vt_c, r_vc = prev
                    O, r_O = Or.next()
                    for c in range(2):
                        for hh in range(2):
                            h = 2 * c + hh
                            E, r_E = Es[hh]
                            P.op("tensor", lambda e, E=E, c=c, h=h: e.matmul(out=O[0:65, h, :], lhsT=vt_p[:, h, :], rhs=E[:, 2 * c, :], start=True, stop=False),
                                 reads=[r_vp, r_E], writes=[r_O])
                            P.op("tensor", lambda e, E=E, c=c, h=h: e.matmul(out=O[0:65, h, :], lhsT=vt_c[:, h, :], rhs=E[:, 2 * c + 1, :], start=False, stop=True),
                                 reads=[r_vc, r_E], writes=[r_O])
                    av = acc[0:65, :, n * span:(n + 1) * span]
                    if d > 1:
                        av = av.rearrange("p h (j d) -> p h j d", d=d)[:, :, :, r]
                    if gi == 0:
                        P.op("vector", lambda e: e.tensor_copy(out=av, in_=O[:]), reads=[r_O], writes=[r_acc])
                    else:
                        P.op("vector", lambda e: e.tensor_tensor(out=av, in0=av, in1=O[:], op=ALU.add),
                             reads=[r_O, r_acc], writes=[r_acc])

                skew(len(blocks), [aA, aB])
                for h in range(4):
                    P.op("scalar", lambda e, h=h: e.activation(out=acc[64:65, h, :], in_=acc[64:65, h, :], func=AF.Ln, bias=epst[64:65, 0:1]), reads=[r_acc, r_epst], writes=[r_acc])
                    P.op("scalar", lambda e, h=h: e.activation(out=acc[64:65, h, :], in_=acc[64:65, h, :], func=AF.Exp, scale=-1.0), reads=[r_acc], writes=[r_acc])
                for h in range(4):
                    for tt in range(NTT):
                        pd, r_pd = M.pp.next()
                        P.op("tensor", lambda e, pd=pd, h=h, tt=tt: e.matmul(out=pd[0:64, :], lhsT=sel[0:65, :], rhs=acc[0:65, h, tt * 512:(tt + 1) * 512], start=True, stop=True),
                             reads=[r_sel, r_acc], writes=[r_pd])
                        P.op("vector", lambda e, pd=pd, h=h, tt=tt: e.tensor_tensor(out=yb[:, h, tt * 512:(tt + 1) * 512], in0=acc[0:64, h, tt * 512:(tt + 1) * 512], in1=pd[0:64, :], op=ALU.mult),
                             reads=[r_pd, r_acc], writes=[r_yb])
                end_phase()

            if stopped[0]:
                return
            with ExitStack() as ph:
                sbf, psf = mk_sb(ph), mk_ps(ph)
                M = Ctx()
                M.wblk = Ring(sbf, "c_wblk", [128, 8, BW], BF16, 7)
                M.pp = Ring(psf, "c_pp", [128, 512], F32, 5)
                M.tmpf = Ring(sbf, "c_tmpf", [128, 512], F32, 4)
                co = sbf("c_co", [128, 2, U], F32); r_co = Res("c_co")
                zd = sbf("c_zd", [128, 30 + U], F32); r_zd = Res("c_zd")
                wd = [load_wblk(M, w, 8), load_wblk(M, w, 9)]
                wa = [load_wblk(M, w, 0), load_wblk(M, w, 1), load_wblk(M, w, 2)]

                def glu_chunk(c):
                    P.op("vector", lambda e: e.tensor_copy(out=zd[:, 0:30], in_=zd_tail[:, c, :]), reads=[r_zdt], writes=[r_zd])
                    for tt in range(NTT):
                        glu_d(M, c, tt, wd, zd, r_zd)
                    P.op("vector", lambda e: e.tensor_copy(out=zd_tail[:, c, :], in_=zd[:, U:U + 30]), reads=[r_zd], writes=[r_zdt])

                def make_taps(c):
                    q = []
                    for hf in range(2):
                        cs = slice(hf * 1024, (hf + 1) * 1024)
                        q.append(lambda cs=cs: P.op("vector", lambda e: e.tensor_scalar(out=co[:, c, cs], in0=zd[:, cs.start:cs.stop], scalar1=dwT[:, c, 0:1], scalar2=None, op0=ALU.mult),
                                                    reads=[r_zd, r_dwT], writes=[r_co]))
                        for k in range(1, 31):
                            q.append(lambda cs=cs, k=k: P.op("vector", lambda e: e.scalar_tensor_tensor(out=co[:, c, cs], in0=zd[:, cs.start + k:cs.stop + k], scalar=dwT[:, c, k:k + 1],
                                                                                                          in1=co[:, c, cs], op0=ALU.mult, op1=ALU.add),
                                                             reads=[r_zd, r_dwT, r_co], writes=[r_co]))
                    return q

                def drain(q, n):
                    for _ in range(min(n, len(q))):
                        q.pop(0)()

                glu_chunk(0)
                taps = make_taps(0)
                zb = sbf("c_zb", [128, 514], F32); r_zb = Res("c_zb")
                for c in range(2):
                    P.op("vector", lambda e, c=c: e.tensor_copy(out=zb[:, 0:2], in_=za_tail[:, c, :]), reads=[r_zat], writes=[r_zb])
                    for tt in range(NTT):
                        mixer_a(M, w, c, tt, zb, r_zb, wa, ya, r_ya)
                        drain(taps, 8)
                    P.op("vector", lambda e, c=c: e.tensor_copy(out=za_tail[:, c, :], in_=zb[:, 0:2]), reads=[r_zb], writes=[r_zat])
                drain(taps, len(taps))
                wu = load_wblk(M, w, 6)
                wsv = load_wblk(M, w, 7)
                glu_chunk(1)
                taps = make_taps(1)
                uT = sbf("c_uT", [128, 2, U], BF16); r_uT = Res("c_uT")
                for c in range(2):
                    for tt in range(NTT):
                        ps, r_ps = proj_fm(M, wu[0], wu[1], c, tt * 512, 512)
                        P.op("scalar", lambda e, ps=ps, c=c, tt=tt: e.copy(out=uT[:, c, tt * 512:(tt + 1) * 512], in_=ps[:]),
                             reads=[r_ps], writes=[r_uT])
                        drain(taps, 2)
                st6 = Ring(sbf, "c_st6", [128, 6], F32, 3)
                mvr = Ring(sbf, "c_mv", [128, 2], F32, 3)
                vnr = Ring(sbf, "c_vn", [128, BW], F32, 2)
                vlr = Ring(sbf, "c_vl", [128, BW], BF16, 3)
                psgr = Ring(psf, "c_psg", [128, 2, 128], F32, 2)
                tmr = Ring(sbf, "c_tm", [128, 2, 128], F32, 2)
                def cA(t, _):
                    ps, r_ps = proj_tm(M, wsv[0], wsv[1], t)
                    s6, r_s6 = st6.next()
                    mv, r_mv = mvr.next()
                    P.op("vector", lambda e: e.bn_stats(out=s6[:], in_=ps[:, 0:BW]), reads=[r_ps], writes=[r_s6])
                    P.op("vector", lambda e: e.bn_aggr(out=mv[:], in_=s6[:]), reads=[r_s6], writes=[r_mv])
                    P.op("vector", lambda e: e.tensor_scalar_add(out=mv[:, 1:2], in0=mv[:, 1:2], scalar1=EPS), reads=[r_mv], writes=[r_mv])
                    P.op("vector", lambda e: e.reciprocal(out=mv[:, 1:2], in_=mv[:, 1:2]), reads=[r_mv], writes=[r_mv])
                    P.op("scalar", lambda e: e.activation(out=mv[:, 1:2], in_=mv[:, 1:2], func=AF.Sqrt), reads=[r_mv], writes=[r_mv])
                    vn, r_vn = vnr.next()
                    P.op("vector", lambda e: e.tensor_scalar(out=vn[:], in0=ps[:, 0:BW], scalar1=mv[:, 0:1], scalar2=mv[:, 1:2],
                                                             op0=ALU.subtract, op1=ALU.mult),
                         reads=[r_ps, r_mv], writes=[r_vn])
                    P.op("vector", lambda e: e.tensor_tensor(out=vn[:], in0=vn[:], in1=slng[:], op=ALU.mult), reads=[r_vn, r_slng], writes=[r_vn])
                    vl, r_vl = vlr.next()
                    P.op("vector", lambda e: e.tensor_tensor(out=vl[:], in0=vn[:], in1=slnb[:], op=ALU.add), reads=[r_vn, r_slnb], writes=[r_vl])
                    return vl, r_vl

                def cB(t, prev):
                    vl, r_vl = prev
                    psg, r_psg = psgr.next()
                    for g in range(4):
                        pb = 64 * (g % 2)
                        P.op("tensor", lambda e, g=g, pb=pb: e.matmul(out=psg[pb:pb + 64, g // 2, :], lhsT=vl[:, 64 * g:64 * g + 64], rhs=wsT[:, g, :], start=True, stop=True),
                             reads=[r_vl, r_wsT], writes=[r_psg])
                    tm, r_tm = tmr.next()
                    for hf in range(2):
                        pb = 64 * hf
                        for cc in range(2):
                            g = 2 * cc + hf
                            P.op("vector", lambda e, pb=pb, cc=cc, g=g: e.tensor_tensor(out=tm[pb:pb + 64, cc, :], in0=psg[pb:pb + 64, cc, :], in1=sgub[pb:pb + 64, g, :], op=ALU.add),
                                 reads=[r_psg, r_sgub], writes=[r_tm])
                    P.op("vector", lambda e: e.tensor_tensor(out=yc[:, :, t * 128:(t + 1) * 128], in0=tm[:], in1=uT[:, :, t * 128:(t + 1) * 128], op=ALU.mult),
                         reads=[r_tm, r_uT], writes=[r_yc])

                skew(NT, [cA, cB], after_step=lambda: drain(taps, 3))
                drain(taps, len(taps))
                for tt in range(NTT):
                    cs = slice(tt * 512, (tt + 1) * 512)
                    pm, r_pm = M.pp.next()
                    pq, r_pq = M.pp.next()
                    for c in range(2):
                        P.op("tensor", lambda e, pm=pm, c=c, cs=cs: e.matmul(out=pm[:], lhsT=onesm[:], rhs=co[:, c, cs], start=(c == 0), stop=(c == 1)),
                             reads=[r_onesm, r_co], writes=[r_pm])
                    sqs = []
                    for c in range(2):
                        sq, r_sq = M.tmpf.next()
                        P.op("scalar", lambda e, sq=sq, c=c, cs=cs: e.activation(out=sq[:], in_=co[:, c, cs], func=AF.Square), reads=[r_co], writes=[r_sq])
                        sqs.append((sq, r_sq))
                    for c in range(2):
                        sq, r_sq = sqs[c]
                        P.op("tensor", lambda e, pq=pq, sq=sq, c=c: e.matmul(out=pq[:], lhsT=onesm[:], rhs=sq[:], start=(c == 0), stop=(c == 1)),
                             reads=[r_onesm, r_sq], writes=[r_pq])
                    m2, r_m2 = M.tmpf.next()
                    P.op("scalar", lambda e, m2=m2, pm=pm: e.activation(out=m2[:], in_=pm[:], func=AF.Square), reads=[r_pm], writes=[r_m2])
                    P.op("vector", lambda e, m2=m2, pq=pq: e.tensor_tensor(out=m2[:], in0=pq[:], in1=m2[:], op=ALU.subtract), reads=[r_pq, r_m2], writes=[r_m2])
                    P.op("vector", lambda e, m2=m2: e.tensor_scalar_add(out=m2[:], in0=m2[:], scalar1=EPS), reads=[r_m2], writes=[r_m2])
                    P.op("vector", lambda e, m2=m2: e.reciprocal(out=m2[:], in_=m2[:]), reads=[r_m2], writes=[r_m2])
                    P.op("scalar", lambda e, m2=m2: e.activation(out=m2[:], in_=m2[:], func=AF.Sqrt), reads=[r_m2], writes=[r_m2])
                    for c in range(2):
                        yn, r_yn = M.tmpf.next()
                        P.op("vector", lambda e, yn=yn, pm=pm, c=c, cs=cs: e.tensor_tensor(out=yn[:], in0=co[:, c, cs], in1=pm[:], op=ALU.subtract),
                             reads=[r_co, r_pm], writes=[r_yn])
                        P.op("vector", lambda e, yn=yn, m2=m2: e.tensor_tensor(out=yn[:], in0=yn[:], in1=m2[:], op=ALU.mult),
                             reads=[r_yn, r_m2], writes=[r_yn])
                        P.op("scalar", lambda e, yn=yn, c=c, cs=cs: e.activation(out=yd[:, c, cs], in_=yn[:], func=AF.Silu, scale=clng[:, c:c + 1], bias=clnb[:, c:c + 1]),
                             reads=[r_yn, r_clng, r_clnb], writes=[r_yd])
                end_phase()

        def phase_merge(w, Y, mT, r_mT):
            ya, r_ya, yb, r_yb, yc, r_yc, yd, r_yd = Y
            if stopped[0]:
                return
            with ExitStack() as ph:
                sbf, psf = mk_sb(ph), mk_ps(ph)
                wbA = sbf("m_wbA", [128, 2, D], BF16); r_wbA = Res("m_wbA")
                wbB = sbf("m_wbB", [64, 4, D], BF16); r_wbB = Res("m_wbB")
                wbC = sbf("m_wbC", [128, 2, D], BF16); r_wbC = Res("m_wbC")
                wbD = sbf("m_wbD", [128, 2, D], BF16); r_wbD = Res("m_wbD")
                P.dma("gpsimd", wbA[:], w["w_branch"][0].rearrange("(c p) n -> p c n", p=128), r_wbA, writes=[r_wbA])
                P.dma("gpsimd", wbB[:], w["w_branch"][1].rearrange("(c p) n -> p c n", p=64), r_wbB, writes=[r_wbB])
                P.dma("gpsimd", wbC[:], w["w_branch"][2].rearrange("(c p) n -> p c n", p=128), r_wbC, writes=[r_wbC])
                P.dma("gpsimd", wbD[:], w["w_branch"][3].rearrange("(c p) n -> p c n", p=128), r_wbD, writes=[r_wbD])
                wgr = Ring(sbf, "m_wg", [128, 8, 128], BF16, 8)
                pgr = Ring(psf, "m_pg", [128, 512], F32, 4)
                pbr = Ring(psf, "m_pb", [128, 512], F32, 4)
                sgr = Ring(sbf, "m_sg", [128, 512], F32, 3)
                tpr = Ring(sbf, "m_tp", [128, 512], F32, 3)
                mac = Ring(sbf, "m_mac", [128, 512], F32, 2)
                branches = ((ya, r_ya, wbA, r_wbA, 2), (yb, r_yb, wbB, r_wbB, 4), (yc, r_yc, wbC, r_wbC, 2), (yd, r_yd, wbD, r_wbD, 2))
                def load_wg(j):
                    wg = []
                    for k in range(4):
                        wt, r_w = wgr.next()
                        P.dma("gpsimd", wt[:], _wsrc(w["w_gate"][k][:, j * 128:(j + 1) * 128]), r_w, writes=[r_w])
                        wg.append((wt, r_w))
                    return wg

                wg_next = load_wg(0)
                for j in range(8):
                    wg = wg_next
                    if j + 1 < 8:
                        wg_next = load_wg(j + 1)
                    for tt in range(NTT):
                        cs = slice(tt * 512, (tt + 1) * 512)
                        ma, r_ma = mac.next()
                        for k in range(4):
                            wt, r_w = wg[k]
                            pg, r_pg = pgr.next()
                            for kc in range(8):
                                P.op("tensor", lambda e, pg=pg, wt=wt, kc=kc, cs=cs: e.matmul(out=pg[:], lhsT=wt[:, kc, :], rhs=hT[:, kc, cs], start=(kc == 0), stop=(kc == 7)),
                                     reads=[r_w, r_hT], writes=[r_pg])
                            yt, r_y, wb, r_wb, nk = branches[k]
                            pb_, r_pb = pbr.next()
                            kp = 128 if nk == 2 else 64
                            for kc in range(nk):
                                P.op("tensor", lambda e, pb_=pb_, wb=wb, yt=yt, kc=kc, kp=kp, nk=nk, cs=cs, j=j: e.matmul(out=pb_[:], lhsT=wb[0:kp, kc, j * 128:(j + 1) * 128], rhs=yt[0:kp, kc, cs], start=(kc == 0), stop=(kc == nk - 1)),
                                     reads=[r_wb, r_y], writes=[r_pb])
                            sg, r_sg = sgr.next()
                            P.op("scalar", lambda e, sg=sg, pg=pg: e.activation(out=sg[:], in_=pg[:], func=AF.Sigmoid), reads=[r_pg], writes=[r_sg])
                            if k == 0:
                                P.op("vector", lambda e, ma=ma, sg=sg, pb_=pb_: e.tensor_tensor(out=ma[:], in0=sg[:], in1=pb_[:], op=ALU.mult),
                                     reads=[r_sg, r_pb], writes=[r_ma])
                            else:
                                tp, r_tp = tpr.next()
                                P.op("vector", lambda e, tp=tp, sg=sg, pb_=pb_: e.tensor_tensor(out=tp[:], in0=sg[:], in1=pb_[:], op=ALU.mult),
                                     reads=[r_sg, r_pb], writes=[r_tp])
                                if k < 3:
                                    P.op("gpsimd", lambda e, ma=ma, tp=tp: e.tensor_tensor(out=ma[:], in0=ma[:], in1=tp[:], op=ALU.add),
                                         reads=[r_ma, r_tp], writes=[r_ma])
                                else:
                                    P.op("gpsimd", lambda e, ma=ma, tp=tp, j=j, cs=cs: e.tensor_tensor(out=mT[:, j, cs], in0=ma[:], in1=tp[:], op=ALU.add),
                                         reads=[r_ma, r_tp], writes=[r_mT])
                end_phase()

        def xloader(ring, src, rs_src, n, cols=None):
            pre = {}

            def ld(t):
                if 0 <= t < n and t not in pre:
                    xt, r_x = ring.next()
                    rows = slice(t * 128, (t + 1) * 128)
                    ap = src[rows, :] if cols is None else src[rows, cols]
                    P.dma("sync", xt[:], ap, r_x, reads=[rs_src[t]], writes=[r_x])
                    pre[t] = (xt, r_x)
                return pre.get(t)
            return ld

        def resid_tile(R, po, r_po, extra, t, src, rs_src, dst, rs_dst):
            xt, r_x = R.ld(t)
            rows = slice(t * 128, (t + 1) * 128)
            xn, r_xn = R.xn.next()
            if extra is None:
                P.op("vector", lambda e: e.tensor_tensor(out=xn[:], in0=po[:], in1=xt[:], op=ALU.add), reads=[r_po, r_x], writes=[r_xn])
            else:
                extra(xn, r_xn, xt, r_x)
            P.dma("sync", dst[rows, :], xn[:], r_xn, reads=[r_xn], writes=[rs_dst[t]])
            return xn, r_xn

        def phase_outproj(w, mT, r_mT, src, rs_src, dst, rs_dst):
            if stopped[0]:
                return
            with ExitStack() as ph:
                sbf, psf = mk_sb(ph), mk_ps(ph)
                R = Ctx()
                R.N = NormBufs(sbf, psf)
                R.xr = Ring(sbf, "o_x", [128, D], F32, 3)
                R.xn = Ring(sbf, "o_xn", [128, D], F32, 3)
                R.ld = xloader(R.xr, src, rs_src, NT)
                wo = sbf("o_w", [128, 8, D], BF16); r_wo = Res("o_w")
                for k2 in range(2):
                    r_h = Res(f"o_w{k2}")
                    P.dma("gpsimd", wo[:, 4 * k2:4 * k2 + 4, :], _wsrc(w["w_out"])[:, 4 * k2:4 * k2 + 4, :], r_h, writes=[r_wo])
                por = Ring(psf, "o_po", [128, D], F32, 2)

                def sA(t, _):
                    R.ld(t)
                    R.ld(t + 1)
                    po, r_po = por.next()
                    for hf in range(2):
                        for k in range(8):
                            P.op("tensor", lambda e, po=po, t=t, hf=hf, k=k: e.matmul(out=po[:, hf * 512:(hf + 1) * 512], lhsT=mT[:, k, t * 128:(t + 1) * 128], rhs=wo[:, k, hf * 512:(hf + 1) * 512], start=(k == 0), stop=(k == 7)),
                                 reads=[r_mT, r_wo], writes=[r_po])
                    xn, r_xn = resid_tile(R, po, r_po, None, t, src, rs_src, dst, rs_dst)
                    ss, r_ss = norm_A(R.N, xn[:], r_xn)
                    return (xn, r_xn, ss, r_ss)

                def sB(t, prev):
                    xn, r_xn, ss, r_ss = prev
                    return norm_B(R.N, xn[:], r_xn, ss, r_ss, gffn, r_gffn)

                def sC(t, prev):
                    norm_C(prev[0], prev[1], t * 128)

                skew(NT, [sA, sB, sC])
                end_phase()

        def phase_ffn(w, src, rs_src, dst, rs_dst):
            if stopped[0]:
                return
            with ExitStack() as ph:
                sbf0 = mk_sb(ph)
                fT = sbf0("f_fT", [128, NJ, U], BF16); r_fT = Res("f_fT")
                w2 = sbf0("g_w2", [128, NJ, 512], BF16); r_w2 = Res("g_w2")
                src_w2 = _wsrc(w["w_ffn_out"])

                def load_w2(hf):
                    for q in range(0, NJ, 6):
                        q1 = min(NJ, q + 6)
                        r_h = Res(f"g_w2_{q}")
                        P.dma("gpsimd", w2[:, q:q1, :], src_w2[:, q:q1, hf * 512:(hf + 1) * 512], r_h, writes=[r_w2])
                if stopped[0]:
                    return
                with ExitStack() as ph1:
                    sbf, psf = mk_sb(ph1), mk_ps(ph1)
                    w1g = Ring(sbf, "f_w1g", [128, 8, 128], BF16, 3)
                    w1u = Ring(sbf, "f_w1u", [128, 8, 128], BF16, 3)
                    pgr = Ring(psf, "f_pg", [128, 512], F32, 4)
                    pur = Ring(psf, "f_pu", [128, 512], F32, 4)
                    sgr = Ring(sbf, "f_sg", [128, 512], F32, 3)
                    for j in range(NJ):
                        wtg, r_wg_ = w1g.next()
                        wtu, r_wu_ = w1u.next()
                        P.dma("gpsimd", wtg[:], _wsrc(w["w_ffn_in"][:, j * 128:(j + 1) * 128]), r_wg_, writes=[r_wg_])
                        P.dma("gpsimd", wtu[:], _wsrc(w["w_ffn_in"][:, FH + j * 128:FH + (j + 1) * 128]), r_wu_, writes=[r_wu_])
                        if j == 2:
                            load_w2(0)
                        for tt in range(NTT):
                            cs = slice(tt * 512, (tt + 1) * 512)
                            pg, r_pg = pgr.next()
                            pu, r_pu = pur.next()
                            for k in range(8):
                                P.op("tensor", lambda e, pg=pg, wtg=wtg, k=k, cs=cs: e.matmul(out=pg[:], lhsT=wtg[:, k, :], rhs=hT[:, k, cs], start=(k == 0), stop=(k == 7)),
                                     reads=[r_wg_, r_hT], writes=[r_pg])
                            for k in range(8):
                                P.op("tensor", lambda e, pu=pu, wtu=wtu, k=k, cs=cs: e.matmul(out=pu[:], lhsT=wtu[:, k, :], rhs=hT[:, k, cs], start=(k == 0), stop=(k == 7)),
                                     reads=[r_wu_, r_hT], writes=[r_pu])
                            sg, r_sg = sgr.next()
                            P.op("scalar", lambda e, sg=sg, pg=pg: e.activation(out=sg[:], in_=pg[:], func=AF.Silu), reads=[r_pg], writes=[r_sg])
                            P.op("vector", lambda e, sg=sg, pu=pu, j=j, cs=cs: e.tensor_tensor(out=fT[:, j, cs], in0=sg[:], in1=pu[:], op=ALU.mult),
                                 reads=[r_sg, r_pu], writes=[r_fT])
                    end_phase()
                if stopped[0]:
                    return
                with ExitStack() as ph2:
                    sbf, psf = mk_sb(ph2), mk_ps(ph2)
                    xr = Ring(sbf, "g_x", [128, 512], F32, 4)
                    por = Ring(psf, "g_po", [128, 512], F32, 4)
                    for hf in range(2):
                        ccs = slice(hf * 512, (hf + 1) * 512)
                        if hf == 1:
                            load_w2(1)
                        ld = xloader(xr, src, rs_src, NT, ccs)
                        for t in range(NT):
                            rows = slice(t * 128, (t + 1) * 128)
                            ld(t)
                            ld(t + 1)
                            po, r_po = por.next()
                            for j in range(NJ):
                                P.op("tensor", lambda e, po=po, t=t, j=j: e.matmul(out=po[:], lhsT=fT[:, j, t * 128:(t + 1) * 128], rhs=w2[:, j, :], start=(j == 0), stop=(j == NJ - 1)),
                                     reads=[r_fT, r_w2], writes=[r_po])
                            xt, r_x = ld(t)
                            P.op("vector", lambda e, xt=xt, po=po: e.tensor_tensor(out=xt[:], in0=po[:], in1=xt[:], op=ALU.add), reads=[r_po, r_x], writes=[r_x])
                            P.dma("sync", dst[rows, ccs], xt[:], r_x, reads=[r_x], writes=[rs_dst[t]])
                    end_phase()

        def alloc_ple_w(sbf, w):
            wg = sbf("p_wg", [128, 8, D], BF16); r_wg = Res("p_wg")
            for k2 in range(2):
                r_h = Res(f"p_wg{k2}")
                P.dma("gpsimd", wg[:, 4 * k2:4 * k2 + 4, :], _wsrc(w["w_ple_gate"])[:, 4 * k2:4 * k2 + 4, :], r_h, writes=[r_wg])
            wp = sbf("p_wp", [128, 2, D], BF16); r_wp = Res("p_wp")
            P.dma("gpsimd", wp[:], _wsrc(w["w_ple_proj"]), r_wp, writes=[r_wp])
            return wg, r_wg, wp, r_wp

        def phase_ple(w, p_rows, src, rs_src, dst, rs_dst, PW):
            if stopped[0]:
                return
            with ExitStack() as ph:
                sbf, psf = mk_sb(ph), mk_ps(ph)
                R = Ctx()
                R.xr = Ring(sbf, "p_x", [128, D], F32, 3)
                R.xn = Ring(sbf, "p_xn", [128, D], F32, 2)
                R.ld = xloader(R.xr, src, rs_src, NT)
                wg, r_wg, wp, r_wp = PW
                pbr = Ring(sbf, "p_pb", [128, PLE], BF16, 2)
                pTr = Ring(sbf, "p_pT", [128, 2, 128], BF16, 2)
                ptr_ = Ring(psf, "p_ptr", [128, 2, 128], BF16, 2)
                pgr = Ring(psf, "p_pg", [128, D], F32, 2)
                ppr = Ring(psf, "p_pp", [128, D], F32, 1)
                sgr = Ring(sbf, "p_sg", [128, D], F32, 2)
                for t in range(NT):
                    rows = slice(t * 128, (t + 1) * 128)
                    R.ld(t)
                    R.ld(t + 1)
                    pb_, r_pb = pbr.next()
                    P.dma("gpsimd", pb_[:], p_rows[rows, :], r_pb, writes=[r_pb])
                    ptp, r_ptp = ptr_.next()
                    for k in range(2):
                        P.op("tensor", lambda e, ptp=ptp, pb_=pb_, k=k: e.transpose(out=ptp[:, k, :], in_=pb_[:, k * 128:(k + 1) * 128], identity=identb[:]),
                             reads=[r_pb, r_identb], writes=[r_ptp])
                    pT, r_pT = pTr.next()
                    P.op("scalar", lambda e, pT=pT, ptp=ptp: e.copy(out=pT[:], in_=ptp[:]), reads=[r_ptp], writes=[r_pT])
                    pg, r_pg = pgr.next()
                    pp_, r_pp = ppr.next()
                    for hf in range(2):
                        for k in range(8):
                            P.op("tensor", lambda e, pg=pg, t=t, hf=hf, k=k: e.matmul(out=pg[:, hf * 512:(hf + 1) * 512], lhsT=hT[:, k, t * 128:(t + 1) * 128], rhs=wg[:, k, hf * 512:(hf + 1) * 512], start=(k == 0), stop=(k == 7)),
                                 reads=[r_hT, r_wg], writes=[r_pg])
                    for hf in range(2):
                        for k in range(2):
                            P.op("tensor", lambda e, pp_=pp_, pT=pT, hf=hf, k=k: e.matmul(out=pp_[:, hf * 512:(hf + 1) * 512], lhsT=pT[:, k, :], rhs=wp[:, k, hf * 512:(hf + 1) * 512], start=(k == 0), stop=(k == 1)),
                                 reads=[r_pT, r_wp], writes=[r_pp])
                    sg, r_sg = sgr.next()
                    P.op("scalar", lambda e, sg=sg, pg=pg: e.activation(out=sg[:], in_=pg[:], func=AF.Sigmoid), reads=[r_pg], writes=[r_sg])

                    def extra(xn, r_xn, xt, r_x, sg=sg, r_sg=r_sg, pp_=pp_, r_pp=r_pp):
                        P.op("vector", lambda e: e.tensor_tensor(out=sg[:], in0=sg[:], in1=pp_[:], op=ALU.mult), reads=[r_sg, r_pp], writes=[r_sg])
                        P.op("vector", lambda e: e.tensor_tensor(out=xn[:], in0=sg[:], in1=xt[:], op=ALU.add), reads=[r_sg, r_x], writes=[r_xn])
                    resid_tile(R, None, None, extra, t, src, rs_src, dst, rs_dst)
                end_phase()

        def phase_final():
            if stopped[0]:
                return
            with ExitStack() as ph:
                sbf = mk_sb(ph)
                gf = sbf("z_g", [128, D], F32); r_gf = Res("z_g")
                P.dma("sync", gf[:], bcast(gfin_in, D), r_gf, writes=[r_gf])
                xr = Ring(sbf, "z_x", [128, D], F32, 6)
                sqr = Ring(sbf, "z_sq", [128, D], BF16, 2)
                ssr = Ring(sbf, "z_ss", [128, 1], F32, 4)

                zrs = [r_X[2 + t // NT][t % NT] for t in range(TC // 128)]
                zld = xloader(xr, y_out, zrs, TC // 128)
                zst = [Res(f"z_st{i}") for i in range(6)]

                def zA(t, _):
                    zld(t); zld(t + 1); zld(t + 2)
                    xt, r_x = zld(t)
                    rows = slice(t * 128, (t + 1) * 128)
                    r_row = zrs[t]
                    sq, r_sq = sqr.next()
                    ss, r_ss = ssr.next()
                    P.op("scalar", lambda e: e.activation(out=sq[:], in_=xt[:], func=AF.Square, scale=1.0 / 32.0, accum_out=ss[:]),
                         reads=[r_x], writes=[r_sq, r_ss])
                    P.op("vector", lambda e: e.tensor_scalar_add(out=ss[:], in0=ss[:], scalar1=EPS), reads=[r_ss], writes=[r_ss])
                    P.op("vector", lambda e: e.reciprocal(out=ss[:], in_=ss[:]), reads=[r_ss], writes=[r_ss])
                    P.op("scalar", lambda e: e.activation(out=ss[:], in_=ss[:], func=AF.Sqrt), reads=[r_ss], writes=[r_ss])
                    return xt, r_x, ss, r_ss, rows, r_row

                def zB(t, prev):
                    xt, r_x, ss, r_ss, rows, r_row = prev
                    P.op("vector", lambda e: e.scalar_tensor_tensor(out=xt[:], in0=xt[:], scalar=ss[:, 0:1], in1=gf[:], op0=ALU.mult, op1=ALU.mult),
                         reads=[r_x, r_ss, r_gf], writes=[r_x])
                    P.dma("gpsimd", y_out[rows, :], xt[:], zst[t % 6], reads=[r_x], writes=[r_row])

                skew(TC // 128, [zA, zB])
                end_phase()

        def dump(name, tile_ap, r_t):
            if name in C.dbg and not stopped[0]:
                with ExitStack() as ph:
                    sbf = mk_sb(ph)
                    shp = C.dbg[name].shape
                    tf = sbf("dbg_t", [shp[0], shp[1]], F32); r_tf = Res("dbg_t")
                    P.op("vector", lambda e: e.tensor_copy(out=tf[:], in_=tile_ap), reads=[r_t], writes=[r_tf])
                    P.dma("sync", C.dbg[name], tf[:], r_tf, reads=[r_tf], writes=[r_dbg])
                    end_phase()

        with ExitStack() as ph:
            sbf = mk_sb(ph)
            zt = sbf("zt", [128, VW], BF16); r_zt = Res("zt")
            P.op("vector", lambda e: e.memset(zt[:], 0.0), writes=[r_zt])
            for t in range(NT):
                r_z = Res(f"zt{t}")
                P.dma("sync", vs[t * 128:(t + 1) * 128, :], zt[:], r_z, reads=[r_zt], writes=[r_vs])
            end_phase()
        for l in range(n_layers):
            w = W[l]
            load_layer_small(w)
            first_full = max(0, l - 1)
            if l >= 2:
                sl = l - 2
                xs_, rs_ = xrows(sl)
                phase_halo(w, xs_, rs_, sl < 2, Kb[0], r_Kb[0], U * (sl + 1))
            else:
                P.op("vector", lambda e: e.memset(Kb[0][:], 0.0), writes=[r_Kb[0]])
                P.op("vector", lambda e: e.memset(za_tail[:], 0.0), writes=[r_zat])
                P.op("vector", lambda e: e.memset(zd_tail[:], 0.0), writes=[r_zdt])
            for i, sl in enumerate(range(first_full, NSLOT)):
                Kprev, r_Kp = Kb[i % 2], r_Kb[i % 2]
                Kcur, r_Kc = Kb[(i + 1) % 2], r_Kb[(i + 1) % 2]
                xs_, rs_ = xrows(sl)
                src_, rsrc_ = xin_rows(sl) if l == 0 else (xs_, rs_)
                phase_norm(src_, rsrc_, NT, gmix, r_gmix, sl < 2)
                if stopped[0]:
                    break
                with ExitStack() as ys:
                    sby = mk_sb(ys)
                    ya = sby("ya", [128, 2, U], BF16); yb = sby("yb", [64, 4, U], BF16)
                    yc = sby("yc", [128, 2, U], BF16); yd = sby("yd", [128, 2, U], BF16)
                    Y = (ya, Res("ya"), yb, Res("yb"), yc, Res("yc"), yd, Res("yd"))
                    phase_mixers(w, sl, Kcur, r_Kc, Kprev, r_Kp, Y)
                    if stopped[0]:
                        break
                    mT = sby("mT", [128, 8, U], BF16); r_mT = Res("mT")
                    phase_merge(w, Y, mT, r_mT)
                    phase_outproj(w, mT, r_mT, src_, rsrc_, xs_, rs_)
                phase_ffn(w, xs_, rs_, xs_, rs_)
                if stopped[0]:
                    break
                p_rows = w["ph"][sl * U:(sl + 1) * U, :] if sl < 2 else w["p"][(sl - 2) * U:(sl - 1) * U, :]
                with ExitStack() as pw:
                    PW = alloc_ple_w(mk_sb(pw), w)
                    phase_norm(xs_, rs_, NT, gple, r_gple)
                    phase_ple(w, p_rows, xs_, rs_, xs_, rs_, PW)
                if stopped[0]:
                    break
            if stopped[0]:
                break
        if final_norm and not stopped[0]:
            phase_final()
        P.full_barrier()
        P.emit()
        C.stats = dict(nsem=P.nsem, ninst=P.ninst)
    return nc, C


_CACHE = {}


def _get_prog(n_layers, final_norm, dbg=False, max_phases=None):
    key = (n_layers, final_norm, dbg, max_phases)
    if key not in _CACHE:
        _CACHE[key] = build_program(n_layers, final_norm, dbg, max_phases)
    return _CACHE[key]


def _consts():
    k = np.arange(128)[:, None]
    q = np.arange(128)[None, :]
    prev = (k >= q).astype(np.float32)
    cur = (k <= q).astype(np.float32)
    matt = np.concatenate([prev, cur, prev, cur], axis=1)
    msgu = (k <= q).astype(np.float32)
    return {"ident": np.eye(128, dtype=np.float32), "matt": np.ascontiguousarray(matt), "msgu": msgu}


def _layer_inputs(inp, l, li):
    f = lambda a: np.ascontiguousarray(a, dtype=np.float32)
    return {
        f"w_in{li}": f(inp["w_in"][l]),
        f"conv_aT{li}": f(inp["conv_a"][l].T),
        f"sgu_ln_g{li}": f(inp["sgu_ln_g"][l][None, :]),
        f"sgu_ln_b{li}": f(inp["sgu_ln_b"][l][None, :]),
        f"sgu_wT{li}": f(np.transpose(inp["sgu_w"][l], (0, 2, 1))),
        f"sgu_b{li}": f(inp["sgu_b"][l].reshape(1, 512)),
        f"conf_dwT{li}": f(inp["conf_dw"][l].T),
        f"conf_ln_g{li}": f(inp["conf_ln_g"][l].reshape(2, 128).T),
        f"conf_ln_b{li}": f(inp["conf_ln_b"][l].reshape(2, 128).T),
        f"w_branch{li}": f(inp["w_branch"][l]),
        f"w_gate{li}": f(inp["w_merge_gate"][l]),
        f"w_out{li}": f(inp["w_out"][l]),
        f"g_mix{li}": f(inp["g_mix"][l][None, :]),
        f"g_ffn{li}": f(inp["g_ffn"][l][None, :]),
        f"g_ple{li}": f(inp["g_ple"][l][None, :]),
        f"w_ffn_in{li}": f(inp["w_ffn_in"][l]),
        f"w_ffn_out{li}": f(inp["w_ffn_out"][l]),
        f"w_ple_gate{li}": f(inp["w_ple_gate"][l]),
        f"w_ple_proj{li}": f(inp["w_ple_proj"][l]),
    }


def _core_inputs(inp, layers, core, shared):
    b, half = core // 2, core % 2
    x = inp["x"]
    m = dict(shared)
    m["x"] = np.ascontiguousarray(x[b, half * TC:(half + 1) * TC], dtype=np.float32)
    if half == 0:
        m["xh"] = np.zeros((2 * U, D), np.float32)
        m["flag"] = np.zeros((128, 1), np.float32)
    else:
        m["xh"] = np.ascontiguousarray(x[b, 0:TC], dtype=np.float32)
        m["flag"] = np.ones((128, 1), np.float32)
    for li, l in enumerate(layers):
        m[f"p{li}"] = np.ascontiguousarray(inp["p"][l, b, half * TC:(half + 1) * TC], dtype=np.float32)
        if half == 0:
            m[f"ph{li}"] = np.zeros((2 * U, PLE), np.float32)
        else:
            m[f"ph{li}"] = np.ascontiguousarray(inp["p"][l, b, 0:TC], dtype=np.float32)
    return m


def _shared_inputs(inp, layers):
    shared = dict(_consts())
    shared["g_final"] = np.ascontiguousarray(inp["g_final"][None, :], dtype=np.float32)
    for li, l in enumerate(layers):
        shared.update(_layer_inputs(inp, l, li))
    return shared


def kernel(**inputs):
    inp = {k: np.asarray(v) for k, v in inputs.items()}
    layers = list(range(DEPTH))
    nc, C = _get_prog(DEPTH, True)
    shared = _shared_inputs(inp, layers)
    in_maps = [_core_inputs(inp, layers, core, shared) for core in range(8)]
    res = run_bass_kernel_spmd(nc, in_maps, core_ids=list(range(8)))
    out = np.empty((BATCH, SEQ, D), np.float32)
    for core in range(8):
        b, half = core // 2, core % 2
        out[b, half * TC:(half + 1) * TC] = res.results[core]["y"]
    return out
```
